# Optimizing a Trainium2 kernel written in Bass

```python
import jax, jax.numpy as jnp
from jax import lax
import numpy as np

D_MODEL = 1024
BATCH = 2
SEQ = 8192
DEPTH = 1

MIX_WIDTH = D_MODEL
POOL_WIDTH = MIX_WIDTH // 2
CONV_WIDTH = MIX_WIDTH - POOL_WIDTH
POOL_WINDOWS = (2, 4, 8, 16)
N_POOL_GROUPS = len(POOL_WINDOWS)
POOL_GROUP = POOL_WIDTH // N_POOL_GROUPS
CONV_HEADS = 8
CONV_K = 3
IN_COLS = 2 * POOL_WIDTH + 4 * CONV_WIDTH
EPS = 1e-6

kernel_name = "hybrid_pool_shortconv_block"


def rmsnorm(x, g):
    xf = x.astype(jnp.float32)
    y = xf * lax.rsqrt(jnp.mean(xf * xf, axis=-1, keepdims=True) + EPS)
    return y.astype(x.dtype) * g


def centred_mean(u, w):
    s = u.shape[1]
    cs = jnp.cumsum(u.astype(jnp.float32), axis=1)
    cs = jnp.pad(cs, ((0, 0), (1, 0), (0, 0)))
    t = jnp.arange(s)
    lo = jnp.clip(t - w // 2, 0, s)
    hi = jnp.clip(t + (w - w // 2), 0, s)
    total = jnp.take(cs, hi, axis=1) - jnp.take(cs, lo, axis=1)
    cnt = (hi - lo).astype(jnp.float32)
    return (total / cnt[None, :, None]).astype(u.dtype)


def pool_mixer(u, pool_w, pool_scale):
    b, s, _ = u.shape
    ug = u.reshape(b, s, N_POOL_GROUPS, POOL_GROUP)
    pooled = jnp.stack(
        [centred_mean(ug[:, :, g, :], w) for g, w in enumerate(POOL_WINDOWS)], axis=2
    ) - ug
    mixed = jnp.einsum('bsgc,gcd->bsgd', pooled, pool_w)
    return mixed.reshape(b, s, POOL_WIDTH) * pool_scale


def gated_short_conv(b_gate, c_gate, v, conv_w, conv_b):
    s = v.shape[1]
    cv = c_gate * v
    p = jnp.pad(cv, ((0, 0), (1, 1), (0, 0)))
    conv = p[:, 0:s] * conv_w[0] + p[:, 1:s + 1] * conv_w[1] + p[:, 2:s + 2] * conv_w[2] + conv_b
    return b_gate * conv


def setup_inputs(seed: int = 0) -> dict:
    key = jax.random.key(seed)
    ks = jax.random.split(key, 10)
    f = jnp.float32
    x = jax.random.normal(ks[0], (BATCH, SEQ, D_MODEL), f)
    norm_g = 1.0 + 0.02 * jax.random.normal(ks[1], (DEPTH, D_MODEL), f)
    w_in = jax.random.normal(ks[2], (DEPTH, D_MODEL, IN_COLS), f) * D_MODEL ** -0.5
    pool_w = jax.random.normal(ks[3], (DEPTH, N_POOL_GROUPS, POOL_GROUP, POOL_GROUP), f) * POOL_GROUP ** -0.5
    pool_scale = 1.0 + 0.1 * jax.random.normal(ks[4], (DEPTH, POOL_WIDTH), f)
    conv_w = jax.random.normal(ks[5], (DEPTH, CONV_K, CONV_WIDTH), f) * CONV_K ** -0.5
    conv_b = 0.01 * jax.random.normal(ks[6], (DEPTH, CONV_WIDTH), f)
    w_out = jax.random.normal(ks[7], (DEPTH, MIX_WIDTH, D_MODEL), f) * MIX_WIDTH ** -0.5
    final_g = 1.0 + 0.02 * jax.random.normal(ks[8], (D_MODEL,), f)
    return {"x": x, "norm_g": norm_g, "w_in": w_in, "pool_w": pool_w,
            "pool_scale": pool_scale, "conv_w": conv_w, "conv_b": conv_b,
            "w_out": w_out, "final_g": final_g}


def reference(x, norm_g, w_in, pool_w, pool_scale, conv_w, conv_b, w_out, final_g):
    splits = np.cumsum([POOL_WIDTH, POOL_WIDTH, CONV_WIDTH, CONV_WIDTH, CONV_WIDTH]).tolist()
    for l in range(DEPTH):
        h = rmsnorm(x, norm_g[l])
        proj = jnp.einsum('bsd,de->bse', h, w_in[l])
        u_a, z_a, b_g, c_g, v_b, z_b = jnp.split(proj, splits, axis=-1)
        y_a = pool_mixer(u_a, pool_w[l], pool_scale[l]) * jax.nn.silu(z_a)
        y_b = gated_short_conv(b_g, c_g, v_b, conv_w[l], conv_b[l]) * jax.nn.silu(z_b)
        y = jnp.concatenate([y_a, y_b], axis=-1)
        x = x + jnp.einsum('bse,ed->bsd', y, w_out[l])
    return rmsnorm(x, final_g)
```

```python
import numpy as np
from contextlib import ExitStack

import concourse.bass as bass
import concourse.mybir as mybir
from concourse.bass_utils import run_bass_kernel_spmd

F32 = mybir.dt.float32
BF16 = mybir.dt.bfloat16
AF = mybir.ActivationFunctionType
ALU = mybir.AluOpType

D = 1024
SEQ = 8192
NCORES = 8
T = 2048
CH = 512
NCH = T // CH
HALO = 8
EXT = CH + 2 * HALO
EPS = 1e-6
WINDOWS = (2, 4, 8, 16)

G0 = 0
PS0 = 8
CW0 = 12
CB0 = 24
RC0 = 28
NH0 = 92
NCST = 96

S_U, S_ZA, S_B, S_C, S_V, S_ZB = range(6)


class Tracker:
    COMPUTE = ("pe", "act", "dve", "pool")

    def __init__(self, nc, stack):
        self.nc = nc
        self.stack = stack
        self.esem = {e: stack.enter_context(nc.semaphore("s_" + e)) for e in self.COMPUTE}
        self.ecount = {e: 0 for e in self.COMPUTE}
        self.streams = {e: [] for e in ("pe", "act", "dve", "pool", "sp")}
        self.lastw = {}
        self.readers = {}
        self.waited = {e: {} for e in self.streams}
        self.dsem = {}

    def op(self, eng, fn, reads=(), writes=(), dma=None, ndma=1):
        if dma is None:
            self.ecount[eng] += 1
            tok = ("eng", eng, self.ecount[eng])
        else:
            if dma not in self.dsem:
                self.dsem[dma] = [self.stack.enter_context(self.nc.semaphore("d_" + dma)), 0]
            ent = self.dsem[dma]
            ent[1] += 16 * ndma
            tok = ("dma", dma, ent[1])
        deps = {}

        def add(t):
            kind, name, val = t
            if kind == "eng" and name == eng and eng == "pe":
                return
            key = (kind, name)
            if deps.get(key, 0) < val:
                deps[key] = val

        for k in reads:
            if k in self.lastw:
                add(self.lastw[k])
        for k in writes:
            if k in self.lastw:
                add(self.lastw[k])
            for r in self.readers.get(k, ()):
                add(r)
        for k in reads:
            self.readers.setdefault(k, []).append(tok)
        for k in writes:
            self.lastw[k] = tok
            self.readers[k] = []
        waits = []
        for key, v in deps.items():
            if self.waited[eng].get(key, 0) >= v:
                continue
            self.waited[eng][key] = v
            waits.append((key[0], key[1], v))
        self.streams[eng].append((waits, fn, tok))
        return tok

    def finalize(self):
        ref = {e: set() for e in self.COMPUTE}
        for st in self.streams.values():
            for waits, fn, tok in st:
                for kind, name, v in waits:
                    if kind == "eng":
                        ref[name].add(v)
        self.rank = {e: {v: i + 1 for i, v in enumerate(sorted(ref[e]))} for e in self.COMPUTE}

    def emit(self, eng_name, eng, final_waits=()):
        for waits, fn, tok in self.streams[eng_name]:
            for kind, name, v in waits:
                if kind == "eng":
                    eng.wait_ge(self.esem[name], self.rank[name][v])
                else:
                    eng.wait_ge(self.dsem[name][0], v)
            if tok[0] == "dma":
                fn(eng, self.dsem[tok[1]][0])
            else:
                inst = fn(eng)
                if tok[2] in self.rank[tok[1]]:
                    inst.then_inc(self.esem[tok[1]], 1)
        for s, v in final_waits:
            eng.wait_ge(s, v)


def build_nc():
    nc = bass.Bass("TRN2", target_bir_lowering=False)
    x = nc.dram_tensor("x", [T + 2 * HALO, D], F32, kind="ExternalInput").ap()
    win = nc.dram_tensor("win", [128, 24, 1024], F32, kind="ExternalInput").ap()
    wout = nc.dram_tensor("wout", [128, 8, 1024], F32, kind="ExternalInput").ap()
    pw = nc.dram_tensor("pw", [128, 512], F32, kind="ExternalInput").ap()
    cst = nc.dram_tensor("cst", [128, NCST], F32, kind="ExternalInput").ap()
    ident = nc.dram_tensor("ident", [128, 128], F32, kind="ExternalInput").ap()
    gfin = nc.dram_tensor("gfin", [128, D], F32, kind="ExternalInput").ap()
    gin = nc.dram_tensor("gin", [1, D], F32, kind="ExternalInput").ap()
    y = nc.dram_tensor("y", [T, D], F32, kind="ExternalOutput").ap()

    with ExitStack() as es:
        def sb(name, shape, dt=F32):
            return es.enter_context(nc.sbuf_tensor(name, shape, dt))

        def ps(name, shape, dt=F32):
            return es.enter_context(nc.psum_tensor(name, shape, dt))

        cst_sb = sb("cst_sb", [128, NCST])
        ident_f = sb("ident_f", [128, 128])
        ident_b = sb("ident_b", [128, 128], BF16)
        gfin_sb = sb("gfin_sb", [128, D])
        gin_sb = sb("gin_sb", [128, D])
        grow = sb("grow", [1, D])
        ones = sb("ones", [1, 128])
        wb = sb("wb", [128, 24, 1024], BF16)
        wo = sb("wo", [128, 8, 1024], BF16)
        pwb = sb("pwb", [128, 512], BF16)
        NXB = 5
        xb = [sb(f"xb{i}", [128, D]) for i in range(NXB)]
        NXS = 4
        xs = [sb(f"xs{i}", [128, D], BF16) for i in range(NXS)]
        junks = [sb(f"junk{i}", [128, D], BF16) for i in range(3)]
        jc = [0]
        ss = sb("ss", [128, 32])
        rst = sb("rst", [128, 32])
        rs = sb("rs", [128, 32])
        ss2 = sb("ss2", [128, 16])
        rst2 = sb("rst2", [128, 16])
        rs2 = sb("rs2", [128, 16])
        hT = [sb(f"hT{i}", [128, 8, EXT], BF16) for i in range(2)]
        u = [sb(f"u{i}", [128, EXT]) for i in range(2)]
        sa = sb("sa", [128, EXT])
        sbb = sb("sbb", [128, EXT])
        sza = [sb(f"sza{i}", [128, CH]) for i in range(2)]
        csb = [sb(f"csb{i}", [128, EXT]) for i in range(2)]
        cv = [sb(f"cv{i}", [128, EXT]) for i in range(2)]
        szb = [sb(f"szb{i}", [128, CH]) for i in range(2)]
        gate = [sb(f"gate{i}", [128, CH]) for i in range(2)]
        t1 = [sb(f"t1{i}", [128, CH]) for i in range(2)]
        t2 = sb("t2", [128, CH])
        fx = sb("fx", [128, 8])
        pd = [sb(f"pd{i}", [128, 4, CH], BF16) for i in range(2)]
        yT = [sb(f"yT{i}", [128, 8, CH], BF16) for i in range(2)]

        Tp = ps("Tp", [128, 8, 128], BF16)
        P = [ps(f"P{i}", [128, CH]) for i in range(3)]
        Hp = ps("Hp", [128, CH])
        Mp = ps("Mp", [128, CH])
        O = [ps(f"O{i}", [128, CH]) for i in range(2)]

        tr = Tracker(nc, es)

        def ld(dst, src, key, name):
            tr.op("sp", lambda e, s: e.dma_start(out=dst, in_=src).then_inc(s, 16),
                  writes=[key], dma=name)

        def cs(col):
            return cst_sb[:, col:col + 1]

        def xload_only(tc_, tile, bi):
            r0 = HALO + tc_ * CH + tile * 128
            tr.op("sp", lambda e, sm: e.dma_start(out=xb[bi][:, :], in_=x[r0:r0 + 128, :]).then_inc(sm, 16),
                  writes=[("xb", bi)], dma=f"xb{bi}")

        def xload_halo(tc_, bi):
            ra = tc_ * CH
            rb = tc_ * CH + CH + HALO

            def f(e, sm):
                e.dma_start(out=xb[bi][0:8, :], in_=x[ra:ra + 8, :]).then_inc(sm, 16)
                e.dma_start(out=xb[bi][8:16, :], in_=x[rb:rb + 8, :]).then_inc(sm, 16)
            tr.op("sp", f, writes=[("xb", bi)], dma=f"xb{bi}", ndma=2)

        def xprep_a(tc, tile, bi, xsi, load=True):
            c = tc * 5 + tile
            n = 128 if tile < 4 else 16
            if load and tile < 4:
                xload_only(tc, tile, bi)
            elif load:
                xload_halo(tc, bi)
            ji = jc[0] % 3
            jc[0] += 1
            tr.op("act", lambda e: e.activation(out=junks[ji][0:n, :], in_=xb[bi][0:n, :], func=AF.Square,
                                                accum_out=ss[0:n, c:c + 1]),
                  reads=[("xb", bi)], writes=[("junk", ji), ("ss", c)])
            tr.op("act", lambda e: e.activation(out=rst[0:n, c:c + 1], in_=ss[0:n, c:c + 1], func=AF.Identity,
                                                bias=EPS, scale=1.0 / D),
                  reads=[("ss", c)], writes=[("rst", c)])
            tr.op("pool", lambda e: e.tensor_tensor(out=rs[0:n, c:c + 1], in0=rst[0:n, c:c + 1],
                                                    in1=cst_sb[0:n, NH0:NH0 + 1], op=ALU.pow),
                  reads=[("rst", c), ("cst",)], writes=[("rs", c)])
            tr.op("dve", lambda e: e.scalar_tensor_tensor(out=xs[xsi][0:n, :], in0=xb[bi][0:n, :],
                                                          scalar=rs[0:n, c:c + 1], in1=gin_sb[0:n, :],
                                                          op0=ALU.mult, op1=ALU.mult),
                  reads=[("xb", bi), ("rs", c), ("gin", 0), ("gin", 1)], writes=[("xs", xsi)])

        Tp32 = Tp[:, :, :].rearrange("p k t -> p (k t)").bitcast(F32)

        def bfview(t):
            return t[:, :].bitcast(BF16).rearrange("p (k t) -> p k t", k=8)

        def xprep_b(tc, tile, xsi, ev="act", bank=None):
            hs = tc % 2
            n = 128 if tile < 4 else 16
            tpv, tkey = (Tp, ("Tp",)) if bank is None else bank

            def ftr(e):
                inst = None
                for k in range(8):
                    inst = e.transpose(out=tpv[:, k, 0:n], in_=xs[xsi][0:n, k * 128:(k + 1) * 128],
                                       identity=ident_b[0:n, 0:n])
                return inst
            tr.op("pe", ftr, reads=[("xs", xsi), ("identb",)], writes=[tkey])
            if tile < 4:
                dst, srcp = hT[hs][:, :, tile * 128:(tile + 1) * 128], tpv[:, :, 0:128]
            else:
                dst, srcp = hT[hs][:, :, CH:CH + 16], tpv[:, :, 0:16]
            if ev == "act":
                tr.op("act", lambda e: e.activation(out=dst, in_=srcp, func=AF.Copy),
                      reads=[tkey], writes=[("hT", hs, tile)])
            else:
                tr.op("dve", lambda e: e.tensor_copy(out=dst, in_=srcp),
                      reads=[tkey], writes=[("hT", hs, tile)])

        def wdma(dst, src, key, name):
            tr.op("pool", lambda e, s: e.dma_start(out=dst, in_=src).then_inc(s, 16),
                  writes=[key], dma=name)

        def wload_in(j, sec):
            q = j * 6 + sec
            wdma(wb[:, q, :], win[:, q, :], ("wb", q), f"wb{q}")

        def wload_out(k):
            wdma(wo[:, k, :], wout[:, k, :], ("wo", k), f"wo{k}")

        def halo_cols(t):
            return bass.AP(t, 0, [[EXT, 128], [HALO + CH, 2], [1, HALO]])

        def halo_src(bank):
            return bank[:, 0:16].rearrange("p (a b) -> p a b", a=2)

        pg = [0]

        def proj_group(tc, j, sec):
            hs = tc % 2
            b = pg[0] % 3
            pg[0] += 1
            q = j * 6 + sec

            def f(e):
                inst = None
                for k in range(8):
                    inst = e.matmul(out=P[b][:, :], lhsT=wb[:, q, k * 128:(k + 1) * 128], rhs=hT[hs][:, k, 0:CH],
                                    start=(k == 0), stop=(k == 7))
                return inst
            tr.op("pe", f, reads=[("wb", q)] + [("hT", hs, t) for t in range(4)], writes=[("P", b)])
            return b

        def halo_group(tc, j, sec, hidx, pos):
            hs = tc % 2
            bank, key = ((Hp, ("Hp",)), (Hp, ("Hp",)), (Tp32, ("Tp",)))[hidx]
            q = j * 6 + sec

            def f(e):
                inst = None
                for k in range(8):
                    inst = e.matmul(out=bank[:, 0:16], lhsT=wb[:, q, k * 128:(k + 1) * 128],
                                    rhs=hT[hs][:, k, CH:CH + 16], start=(k == 0), stop=(k == 7))
                return inst
            tr.op("pe", f, reads=[("wb", q), ("hT", hs, 4)], writes=[key])
            return bank, key

        pending_pw = [None]

        def do_pending_pw():
            if pending_pw[0] is None:
                return
            tc, g, gs = pending_pw[0]
            pending_pw[0] = None
            pslot = tc % 2
            tr.op("pe", lambda e: e.matmul(out=Mp[:, :], lhsT=pwb[:, g * 128:(g + 1) * 128], rhs=pd[pslot][:, g, :],
                                           start=True, stop=True),
                  reads=[("pwb",), ("pd", pslot, g)], writes=[("Mp",)])
            tr.op("dve", lambda e: e.scalar_tensor_tensor(out=yT[pslot][:, g, :], in0=Mp[:, :], scalar=cs(PS0 + g),
                                                          in1=sza[gs][:, :], op0=ALU.mult, op1=ALU.mult),
                  reads=[("Mp",), ("sza", gs), ("cst",)], writes=[("yT", pslot, g)])

        def proj_block(tc, pos, mid=None, late=None, mid2=None, pd_early=False):
            j = 3 - pos
            js = pos % 2
            pslot = tc % 2
            b = proj_group(tc, j, S_ZA)
            tr.op("act", lambda e, b=b: e.activation(out=sza[js][:, :], in_=P[b][:, :], func=AF.Silu),
                  reads=[("P", b)], writes=[("sza", js)])
            b = proj_group(tc, j, S_U)
            r = halo_group(tc, j, S_U, 0, pos)
            tr.op("act", lambda e, b=b: e.activation(out=u[js][:, HALO:HALO + CH], in_=P[b][:, :], func=AF.Copy),
                  reads=[("P", b)], writes=[("u", js, "m")])
            tr.op("act", lambda e, r=r: e.activation(out=halo_cols(u[js]), in_=halo_src(r[0]), func=AF.Copy),
                  reads=[r[1]], writes=[("u", js, "h")])
            w = WINDOWS[j]
            n = j + 1
            lo = HALO - w // 2
            his = [0] * (n + 2)
            his[n] = HALO + CH - w // 2
            for i in range(n - 1, 0, -1):
                his[i] = his[i + 1] + (1 << i)
            ukeys = [("u", js, "m"), ("u", js, "h")]
            src, skeys = u[js], ukeys
            bufs = [(sa, ("sa",)), (sbb, ("sbb",))]
            for i in range(1, n + 1):
                dst, dkey = bufs[(i - 1) % 2]
                sh = 1 << (i - 1)
                hi = his[i]
                tr.op("pool", lambda e, dst=dst, src=src, sh=sh, hi=hi: e.tensor_tensor(
                    out=dst[:, lo:hi], in0=src[:, lo:hi], in1=src[:, lo + sh:hi + sh], op=ALU.add),
                    reads=list(skeys), writes=[dkey])
                src, skeys = dst, [dkey]
            fin, fkeys = src, skeys
            def emit_pd():
                tr.op("dve", lambda e: e.scalar_tensor_tensor(out=pd[pslot][:, j, :], in0=fin[:, lo:lo + CH],
                                                              scalar=1.0 / w, in1=u[js][:, HALO:HALO + CH],
                                                              op0=ALU.mult, op1=ALU.subtract),
                      reads=list(fkeys) + ukeys, writes=[("pd", pslot, j)])
                for edge, cond, c0 in ((0, tc == 0, 0), (1, tc == NCH - 1, CH - 8)):
                    if not cond:
                        continue
                    rc0 = RC0 + j * 16 + edge * 8
                    tr.op("dve", lambda e, c0=c0, rc0=rc0: e.tensor_tensor(
                        out=fx[:, :], in0=fin[:, lo + c0:lo + c0 + 8], in1=cst_sb[:, rc0:rc0 + 8], op=ALU.mult),
                        reads=list(fkeys) + [("cst",)], writes=[("fx",)])
                    tr.op("dve", lambda e, c0=c0: e.tensor_tensor(
                        out=pd[pslot][:, j, c0:c0 + 8], in0=fx[:, :], in1=u[js][:, HALO + c0:HALO + c0 + 8],
                        op=ALU.subtract),
                        reads=[("fx",)] + ukeys + [("pd", pslot, j)], writes=[("pd", pslot, j)])
            if pd_early:
                emit_pd()
            if mid is not None:
                mid()
            b = proj_group(tc, j, S_ZB)
            tr.op("act", lambda e, b=b: e.activation(out=szb[js][:, :], in_=P[b][:, :], func=AF.Silu),
                  reads=[("P", b)], writes=[("szb", js)])
            b = proj_group(tc, j, S_C)
            r = halo_group(tc, j, S_C, 1, pos)
            tr.op("act", lambda e, b=b: e.activation(out=csb[js][:, HALO:HALO + CH], in_=P[b][:, :], func=AF.Copy),
                  reads=[("P", b)], writes=[("csb", js, "m")])
            tr.op("act", lambda e, r=r: e.activation(out=halo_cols(csb[js]), in_=halo_src(r[0]), func=AF.Copy),
                  reads=[r[1]], writes=[("csb", js, "h")])
            if mid2 is not None:
                mid2()
            b = proj_group(tc, j, S_V)
            r = halo_group(tc, j, S_V, 2, pos)
            tr.op("dve", lambda e, b=b: e.tensor_tensor(out=cv[js][:, HALO:HALO + CH], in0=P[b][:, :],
                                                   in1=csb[js][:, HALO:HALO + CH], op=ALU.mult),
                  reads=[("P", b), ("csb", js, "m")], writes=[("cv", js, "m")])
            tr.op("dve", lambda e, r=r: e.tensor_tensor(out=halo_cols(cv[js]), in0=halo_src(r[0]),
                                                   in1=halo_cols(csb[js]), op=ALU.mult),
                  reads=[r[1], ("csb", js, "h")], writes=[("cv", js, "h")])
            cvkeys = [("cv", js, "m"), ("cv", js, "h")]
            b = proj_group(tc, j, S_B)
            tr.op("dve", lambda e, b=b: e.tensor_tensor(out=gate[js][:, :], in0=P[b][:, :], in1=szb[js][:, :],
                                                   op=ALU.mult),
                  reads=[("P", b), ("szb", js)], writes=[("gate", js)])
            tr.op("act", lambda e: e.activation(out=t1[js][:, :], in_=cv[js][:, HALO:HALO + CH], func=AF.Identity,
                                                bias=cs(CB0 + j), scale=cs(CW0 + 4 + j)),
                  reads=cvkeys + [("cst",)], writes=[("t1", js)])
            tr.op("dve", lambda e: e.scalar_tensor_tensor(out=t2[:, :], in0=cv[js][:, HALO - 1:HALO - 1 + CH],
                                                          scalar=cs(CW0 + j), in1=t1[js][:, :],
                                                          op0=ALU.mult, op1=ALU.add),
                  reads=cvkeys + [("t1", js), ("cst",)], writes=[("t2",)])
            tr.op("dve", lambda e: e.scalar_tensor_tensor(out=t1[js][:, :], in0=cv[js][:, HALO + 1:HALO + 1 + CH],
                                                          scalar=cs(CW0 + 8 + j), in1=t2[:, :],
                                                          op0=ALU.mult, op1=ALU.add),
                  reads=cvkeys + [("t2",), ("cst",)], writes=[("t1", js)])
            tr.op("dve", lambda e: e.tensor_tensor(out=yT[pslot][:, 4 + j, :], in0=t1[js][:, :], in1=gate[js][:, :],
                                                   op=ALU.mult),
                  reads=[("t1", js), ("gate", js)], writes=[("yT", pslot, 4 + j)])
            if not pd_early:
                emit_pd()
            if late is not None:
                late()
            pending_pw[0] = (tc, j, js)

        pending_store = [None]

        def do_pending_store():
            if pending_store[0] is None:
                return
            i, s = pending_store[0]
            pending_store[0] = None
            tr.op("act", lambda e, sm: e.dma_start(out=y[i * 128:(i + 1) * 128, :], in_=xb[s][:, :]).then_inc(sm, 16),
                  reads=[("xb", s)], writes=[("y", i)], dma=f"yo{s}")

        def out_slot(i):
            if i >= 12:
                return (0, 1, 2, 3)[i - 12]
            return 2 + (i % 3)

        loaded = set()

        def out_load(i):
            if i in loaded or i >= 4 * NCH:
                return
            loaded.add(i)
            s = out_slot(i)
            r0 = HALO + i * 128
            tr.op("sp", lambda e, sm: e.dma_start(out=xb[s][:, :], in_=x[r0:r0 + 128, :]).then_inc(sm, 16),
                  writes=[("xb", s)], dma=f"xb{s}")

        pending_ew = []

        def out_tile_pe(tc, ti, banks=None):
            i = tc * 4 + ti
            ys = tc % 2
            if banks is None:
                banks = ((O[0], ("O", 0)), (O[1], ("O", 1)))

            rd = [("yT", ys, e_) for e_ in range(8)] + [("wo", k) for k in range(8)]
            if i == 4 * NCH - 1:
                for dh in range(2):
                    def fh(e, dh=dh):
                        inst = None
                        for ec in range(8):
                            inst = e.matmul(out=banks[dh][0][:, :], lhsT=yT[ys][:, ec, ti * 128:(ti + 1) * 128],
                                            rhs=wo[:, ec, dh * CH:(dh + 1) * CH], start=(ec == 0), stop=(ec == 7))
                        return inst
                    tr.op("pe", fh, reads=rd, writes=[banks[dh][1]])
            else:
                def f(e):
                    inst = None
                    for ec in range(8):
                        for dh in range(2):
                            inst = e.matmul(out=banks[dh][0][:, :], lhsT=yT[ys][:, ec, ti * 128:(ti + 1) * 128],
                                            rhs=wo[:, ec, dh * CH:(dh + 1) * CH], start=(ec == 0), stop=(ec == 7))
                    return inst
                tr.op("pe", f, reads=rd, writes=[banks[0][1], banks[1][1]])
            pending_ew.append((i, banks))

        pending_ewb = []

        def ew_a():
            if not pending_ew:
                return
            i, banks = pending_ew.pop(0)
            s = out_slot(i)
            for dh in range(2):
                tr.op("dve", lambda e, dh=dh: e.tensor_tensor(out=xb[s][:, dh * CH:(dh + 1) * CH],
                                                              in0=banks[dh][0][:, :],
                                                              in1=xb[s][:, dh * CH:(dh + 1) * CH], op=ALU.add),
                      reads=[banks[dh][1], ("xb", s)], writes=[("xb", s)])
            ji = jc[0] % 3
            jc[0] += 1
            tr.op("act", lambda e: e.activation(out=junks[ji][:, :], in_=xb[s][:, :], func=AF.Square,
                                                accum_out=ss2[:, i:i + 1]),
                  reads=[("xb", s)], writes=[("junk", ji), ("ss2", i)])
            tr.op("pool", lambda e: e.tensor_scalar(rst2[:, i:i + 1], ss2[:, i:i + 1], 1.0 / D, EPS,
                                                    ALU.mult, ALU.add),
                  reads=[("ss2", i)], writes=[("rst2", i)])
            tr.op("pool", lambda e: e.tensor_tensor(out=rs2[:, i:i + 1], in0=rst2[:, i:i + 1],
                                                    in1=cst_sb[:, NH0:NH0 + 1], op=ALU.pow),
                  reads=[("rst2", i), ("cst",)], writes=[("rs2", i)])
            pending_ewb.append(i)

        def ew_b():
            if not pending_ewb:
                return
            i = pending_ewb.pop(0)
            s = out_slot(i)
            tr.op("dve", lambda e: e.scalar_tensor_tensor(out=xb[s][:, :], in0=xb[s][:, :], scalar=rs2[:, i:i + 1],
                                                          in1=gfin_sb[:, :], op0=ALU.mult, op1=ALU.mult),
                  reads=[("xb", s), ("rs2", i), ("gfin",)], writes=[("xb", s)])
            do_pending_store()
            pending_store[0] = (i, s)

        def flush_ew():
            ew_a()
            ew_b()

        warm = sb("warm", [128, 8])
        tr.op("pool", lambda e: e.memset(warm[:, :], 0.0), writes=[("warm",)])
        tr.op("act", lambda e: e.activation(out=warm[:, :], in_=warm[:, :], func=AF.Silu),
              reads=[("warm",)], writes=[("warm",)])
        worder = [S_ZA, S_U, S_ZB, S_C, S_V, S_B]
        gorder = [3, 2, 1, 0]

        ld(grow[:, :], gin[:, :], ("grow",), "grow")
        xload_only(0, 0, 0)
        ld(cst_sb[:, :], cst[:, :], ("cst",), "cst")
        ld(ident_f[:, :], ident[:, :], ("identf",), "identf")
        xload_only(0, 1, 1)
        xload_halo(0, 4)
        xload_only(0, 2, 2)
        xload_only(0, 3, 3)
        tr.op("pool", lambda e: e.memset(ones[:, :], 1.0), writes=[("ones",)])
        for hh in range(2):
            tr.op("pe", lambda e, hh=hh: e.matmul(out=P[hh][:, :], lhsT=ones[0:1, :],
                                                  rhs=grow[0:1, hh * CH:(hh + 1) * CH], start=True, stop=True),
                  reads=[("ones",), ("grow",)], writes=[("P", hh)])
            tr.op("dve", lambda e, hh=hh: e.tensor_copy(out=gin_sb[:, hh * CH:(hh + 1) * CH], in_=P[hh][:, :]),
                  reads=[("P", hh)], writes=[("gin", hh)])
        xprep_a(0, 0, 0, 0, load=False)
        tr.op("act", lambda e: e.activation(out=ident_b[:, :], in_=ident_f[:, :], func=AF.Copy),
              reads=[("identf",)], writes=[("identb",)])
        xprep_a(0, 1, 1, 1, load=False)
        xprep_b(0, 0, 0, "act")
        xprep_a(0, 4, 4, 0, load=False)
        q0 = gorder[0] * 6 + worder[0]
        tr.op("pool", lambda e, s: e.dma_start(out=wb[:, q0, :], in_=win[:, q0, :]).then_inc(s, 16),
              reads=[("xb", 2)], writes=[("wb", q0)], dma=f"wb{q0}")
        xprep_a(0, 2, 2, 2, load=False)
        xprep_b(0, 1, 1, "act", (bfview(O[0]), ("O", 0)))
        xprep_a(0, 3, 3, 3, load=False)
        first = True
        for sec in worder[1:]:
            q = gorder[0] * 6 + sec
            tr.op("pool", lambda e, s, q=q: e.dma_start(out=wb[:, q, :], in_=win[:, q, :]).then_inc(s, 16),
                  reads=[("xb", 3)] if first else [], writes=[("wb", q)], dma=f"wb{q}")
            first = False
        wdma(pwb[:, :], pw[:, :], ("pwb",), "pwb")
        xprep_b(0, 4, 0, "dve", (bfview(O[1]), ("O", 1)))
        xprep_b(0, 2, 2, "act", (bfview(Hp), ("Hp",)))
        xprep_b(0, 3, 3, "dve")
        ld(gfin_sb[:, :], gfin[:, :], ("gfin",), "gfin")

        def btiles(bidx):
            tc_, j_ = divmod(bidx, 4)
            if tc_ >= NCH - 1 or bidx < 0:
                return []
            return [[], [(tc_ + 1, 0), (tc_ + 1, 4)], [(tc_ + 1, 1), (tc_ + 1, 2)], [(tc_ + 1, 3)]][j_]

        xbc = [0]
        xsc = [1]
        prepared = {}

        def do_a(tiles):
            for (tc_, tile) in tiles:
                bi = xbc[0] % 2
                xbc[0] += 1
                xsi = xsc[0] % NXS
                xsc[0] += 1
                xprep_a(tc_, tile, bi, xsi)
                prepared[(tc_, tile)] = xsi

        do_a(btiles(0))
        for tc in range(NCH):
            for j in range(4):
                bidx = tc * 4 + j
                do_a(btiles(bidx + 1))
                if tc >= 1:
                    out_load((tc - 1) * 4 + j)
                if bidx == 14:
                    out_load(12)
                    out_load(13)
                if tc == 0 and j < 3:
                    for sec in worder:
                        wload_in(gorder[j + 1], sec)
                if tc == 0 and j == 3:
                    for k in range(8):
                        wload_out(k)

                def mid(bidx=bidx):
                    if pending_pw[0] is not None and pending_pw[0][0] not in (0, NCH - 1):
                        do_pending_pw()
                    for (tc_, tile) in btiles(bidx)[:1]:
                        xprep_b(tc_, tile, prepared[(tc_, tile)])

                def mid2(bidx=bidx):
                    if pending_pw[0] is not None and (pending_pw[0][1] == 0 or bidx == 4 * NCH - 1):
                        do_pending_pw()
                    for (tc_, tile) in btiles(bidx)[1:]:
                        xprep_b(tc_, tile, prepared[(tc_, tile)])
                    flush_ew()
                    if bidx == 15:
                        out_load(14)
                last = (bidx == 4 * NCH - 1)
                proj_block(tc, j, mid, do_pending_pw, mid2, pd_early=last)
                if last:
                    do_pending_pw()
                if tc >= 1:
                    out_tile_pe(tc - 1, j)
        do_pending_pw()
        tail_banks = [
            ((P[0], ("P", 0)), (P[1], ("P", 1))),
            ((P[2], ("P", 2)), (Hp, ("Hp",))),
            ((O[0], ("O", 0)), (O[1], ("O", 1))),
            ((P[0], ("P", 0)), (P[1], ("P", 1))),
        ]
        flush_ew()
        out_load(15)
        for ti in range(3):
            out_tile_pe(NCH - 1, ti, tail_banks[ti])
        ew_a()
        out_tile_pe(NCH - 1, 3, tail_banks[3])
        ew_a()
        ew_b()
        ew_a()
        ew_b()
        ew_a()
        ew_b()
        ew_b()
        do_pending_store()

        final_waits = [(ent[0], ent[1]) for name, ent in tr.dsem.items() if name.startswith("yo")]
        tr.finalize()


        with nc.Block() as block:
            @block.sync
            def _(e):
                tr.emit("sp", e)

            @block.tensor
            def _(e):
                tr.emit("pe", e)

            @block.scalar
            def _(e):
                tr.emit("act", e, final_waits)

            @block.vector
            def _(e):
                tr.emit("dve", e)

            @block.gpsimd
            def _(e):
                tr.emit("pool", e)
    return nc


_NC_CACHE = {}


def _layout_inputs(x, norm_g, w_in, pool_w, pool_scale, conv_w, conv_b, w_out, final_g):
    f = np.float32
    x = np.asarray(x, f)
    w = np.asarray(w_in, f)[0]
    w = w.reshape(8, 128, 6, 4, 128)
    win = np.ascontiguousarray(w.transpose(1, 3, 2, 0, 4)).reshape(128, 24, 1024)
    wout = np.ascontiguousarray(np.asarray(w_out, f)[0].reshape(8, 128, 1024).transpose(1, 0, 2))
    pw = np.ascontiguousarray(np.asarray(pool_w, f)[0].transpose(1, 0, 2)).reshape(128, 512)
    gin = np.ascontiguousarray(np.asarray(norm_g, f)[0][None, :])
    ident = np.eye(128, dtype=f)
    gfin = np.ascontiguousarray(np.broadcast_to(np.asarray(final_g, f)[None, :], (128, D)))
    cst_base = np.zeros((128, NCST), f)
    cst_base[:, G0:G0 + 8] = np.asarray(norm_g, f)[0].reshape(8, 128).T
    cst_base[:, PS0:PS0 + 4] = np.asarray(pool_scale, f)[0].reshape(4, 128).T
    cw = np.asarray(conv_w, f)[0].reshape(3, 4, 128)
    cst_base[:, CW0:CW0 + 12] = cw.transpose(2, 0, 1).reshape(128, 12)
    cst_base[:, CB0:CB0 + 4] = np.asarray(conv_b, f)[0].reshape(4, 128).T
    cst_base[:, NH0] = -0.5
    in_maps = []
    for c in range(NCORES):
        b, q = divmod(c, SEQ // T)
        t0 = q * T
        xe = np.zeros((T + 2 * HALO, D), f)
        lo = max(t0 - HALO, 0)
        hi = min(t0 + T + HALO, SEQ)
        xe[lo - (t0 - HALO):hi - (t0 - HALO)] = x[b, lo:hi]
        cst = cst_base.copy()
        for g, wdw in enumerate(WINDOWS):
            for edge in range(2):
                for i in range(8):
                    t = t0 + i if edge == 0 else t0 + T - 8 + i
                    cnt = min(t + wdw // 2, SEQ) - max(t - wdw // 2, 0)
                    cst[:, RC0 + g * 16 + edge * 8 + i] = f(1.0) / f(cnt)
        in_maps.append({"x": xe, "win": win, "wout": wout, "pw": pw, "cst": cst,
                        "ident": ident, "gfin": gfin, "gin": gin})
    return in_maps


def kernel(x, norm_g, w_in, pool_w, pool_scale, conv_w, conv_b, w_out, final_g):
    in_maps = _layout_inputs(x, norm_g, w_in, pool_w, pool_scale, conv_w, conv_b, w_out, final_g)
    if "nc" not in _NC_CACHE:
        _NC_CACHE["nc"] = build_nc()
    nc = _NC_CACHE["nc"]
    res = run_bass_kernel_spmd(nc, in_maps, core_ids=list(range(NCORES)))
    out = np.empty((2, SEQ, D), np.float32)
    for c in range(NCORES):
        b, q = divmod(c, SEQ // T)
        out[b, q * T:(q + 1) * T] = res.results[c]["y"]
    return out
```

```python
import numpy as np
from contextlib import ExitStack

import concourse.bass as bass
import concourse.mybir as mybir
from concourse.bass_utils import run_bass_kernel_spmd

F32 = mybir.dt.float32
BF16 = mybir.dt.bfloat16
AF = mybir.ActivationFunctionType
ALU = mybir.AluOpType

D = 1024
SEQ = 8192
NCORES = 8
T = 2048
CH = 512
NCH = T // CH
HALO = 8
EXT = CH + 2 * HALO
EPS = 1e-6
WINDOWS = (2, 4, 8, 16)

G0 = 0
PS0 = 8
CW0 = 12
CB0 = 24
RC0 = 28
NH0 = 92
NCST = 96

S_U, S_ZA, S_B, S_C, S_V, S_ZB = range(6)


class Tracker:
    COMPUTE = ("pe", "act", "dve", "pool")

    def __init__(self, nc, stack):
        self.nc = nc
        self.stack = stack
        self.esem = {e: stack.enter_context(nc.semaphore("s_" + e)) for e in self.COMPUTE}
        self.ecount = {e: 0 for e in self.COMPUTE}
        self.streams = {e: [] for e in ("pe", "act", "dve", "pool", "sp")}
        self.lastw = {}
        self.readers = {}
        self.waited = {e: {} for e in self.streams}
        self.dsem = {}

    def op(self, eng, fn, reads=(), writes=(), dma=None, ndma=1):
        if dma is None:
            self.ecount[eng] += 1
            tok = ("eng", eng, self.ecount[eng])
        else:
            if dma not in self.dsem:
                self.dsem[dma] = [self.stack.enter_context(self.nc.semaphore("d_" + dma)), 0]
            ent = self.dsem[dma]
            ent[1] += 16 * ndma
            tok = ("dma", dma, ent[1])
        deps = {}

        def add(t):
            kind, name, val = t
            if kind == "eng" and name == eng and eng == "pe":
                return
            key = (kind, name)
            if deps.get(key, 0) < val:
                deps[key] = val

        for k in reads:
            if k in self.lastw:
                add(self.lastw[k])
        for k in writes:
            if k in self.lastw:
                add(self.lastw[k])
            for r in self.readers.get(k, ()):
                add(r)
        for k in reads:
            self.readers.setdefault(k, []).append(tok)
        for k in writes:
            self.lastw[k] = tok
            self.readers[k] = []
        waits = []
        for key, v in deps.items():
            if self.waited[eng].get(key, 0) >= v:
                continue
            self.waited[eng][key] = v
            waits.append((key[0], key[1], v))
        self.streams[eng].append((waits, fn, tok))
        return tok

    def finalize(self):
        ref = {e: set() for e in self.COMPUTE}
        for st in self.streams.values():
            for waits, fn, tok in st:
                for kind, name, v in waits:
                    if kind == "eng":
                        ref[name].add(v)
        self.rank = {e: {v: i + 1 for i, v in enumerate(sorted(ref[e]))} for e in self.COMPUTE}

    def emit(self, eng_name, eng, final_waits=()):
        for waits, fn, tok in self.streams[eng_name]:
            for kind, name, v in waits:
                if kind == "eng":
                    eng.wait_ge(self.esem[name], self.rank[name][v])
                else:
                    eng.wait_ge(self.dsem[name][0], v)
            if tok[0] == "dma":
                fn(eng, self.dsem[tok[1]][0])
            else:
                inst = fn(eng)
                if tok[2] in self.rank[tok[1]]:
                    inst.then_inc(self.esem[tok[1]], 1)
        for s, v in final_waits:
            eng.wait_ge(s, v)


def build_nc():
    nc = bass.Bass("TRN2", target_bir_lowering=False)
    x = nc.dram_tensor("x", [T + 2 * HALO, D], F32, kind="ExternalInput").ap()
    win = nc.dram_tensor("win", [128, 24, 1024], F32, kind="ExternalInput").ap()
    wout = nc.dram_tensor("wout", [128, 8, 1024], F32, kind="ExternalInput").ap()
    pw = nc.dram_tensor("pw", [128, 512], F32, kind="ExternalInput").ap()
    cst = nc.dram_tensor("cst", [128, NCST], F32, kind="ExternalInput").ap()
    ident = nc.dram_tensor("ident", [128, 128], F32, kind="ExternalInput").ap()
    gfin = nc.dram_tensor("gfin", [128, D], F32, kind="ExternalInput").ap()
    gin = nc.dram_tensor("gin", [1, D], F32, kind="ExternalInput").ap()
    y = nc.dram_tensor("y", [T, D], F32, kind="ExternalOutput").ap()

    with ExitStack() as es:
        def sb(name, shape, dt=F32):
            return es.enter_context(nc.sbuf_tensor(name, shape, dt))

        def ps(name, shape, dt=F32):
            return es.enter_context(nc.psum_tensor(name, shape, dt))

        cst_sb = sb("cst_sb", [128, NCST])
        ident_f = sb("ident_f", [128, 128])
        ident_b = sb("ident_b", [128, 128], BF16)
        gfin_sb = sb("gfin_sb", [128, D])
        gin_sb = sb("gin_sb", [128, D])
        grow = sb("grow", [1, D])
        ones = sb("ones", [1, 128])
        wb = sb("wb", [128, 24, 1024], BF16)
        wo = sb("wo", [128, 8, 1024], BF16)
        pwb = sb("pwb", [128, 512], BF16)
        NXB = 5
        xb = [sb(f"xb{i}", [128, D]) for i in range(NXB)]
        NXS = 4
        xs = [sb(f"xs{i}", [128, D], BF16) for i in range(NXS)]
        junks = [sb(f"junk{i}", [128, D], BF16) for i in range(3)]
        jc = [0]
        ss = sb("ss", [128, 32])
        rst = sb("rst", [128, 32])
        rs = sb("rs", [128, 32])
        ss2 = sb("ss2", [128, 16])
        rst2 = sb("rst2", [128, 16])
        rs2 = sb("rs2", [128, 16])
        hT = [sb(f"hT{i}", [128, 8, EXT], BF16) for i in range(2)]
        u = [sb(f"u{i}", [128, EXT]) for i in range(2)]
        sa = sb("sa", [128, EXT])
        sbb = sb("sbb", [128, EXT])
        sza = [sb(f"sza{i}", [128, CH]) for i in range(2)]
        csb = [sb(f"csb{i}", [128, EXT]) for i in range(2)]
        cv = [sb(f"cv{i}", [128, EXT]) for i in range(2)]
        szb = [sb(f"szb{i}", [128, CH]) for i in range(2)]
        gate = [sb(f"gate{i}", [128, CH]) for i in range(2)]
        t1 = [sb(f"t1{i}", [128, CH]) for i in range(2)]
        t2 = sb("t2", [128, CH])
        fx = sb("fx", [128, 8])
        pd = [sb(f"pd{i}", [128, 4, CH], BF16) for i in range(2)]
        yT = [sb(f"yT{i}", [128, 8, CH], BF16) for i in range(2)]

        Tp = ps("Tp", [128, 8, 128], BF16)
        P = [ps(f"P{i}", [128, CH]) for i in range(3)]
        Hp = ps("Hp", [128, CH])
        Mp = ps("Mp", [128, CH])
        O = [ps(f"O{i}", [128, CH]) for i in range(2)]

        tr = Tracker(nc, es)

        def ld(dst, src, key, name):
            tr.op("sp", lambda e, s: e.dma_start(out=dst, in_=src).then_inc(s, 16),
                  writes=[key], dma=name)

        def cs(col):
            return cst_sb[:, col:col + 1]

        def xload_only(tc_, tile, bi):
            r0 = HALO + tc_ * CH + tile * 128
            tr.op("sp", lambda e, sm: e.dma_start(out=xb[bi][:, :], in_=x[r0:r0 + 128, :]).then_inc(sm, 16),
                  writes=[("xb", bi)], dma=f"xb{bi}")

        def xload_halo(tc_, bi):
            ra = tc_ * CH
            rb = tc_ * CH + CH + HALO

            def f(e, sm):
                e.dma_start(out=xb[bi][0:8, :], in_=x[ra:ra + 8, :]).then_inc(sm, 16)
                e.dma_start(out=xb[bi][8:16, :], in_=x[rb:rb + 8, :]).then_inc(sm, 16)
            tr.op("sp", f, writes=[("xb", bi)], dma=f"xb{bi}", ndma=2)

        def xprep_a(tc, tile, bi, xsi, load=True):
            c = tc * 5 + tile
            n = 128 if tile < 4 else 16
            if load and tile < 4:
                xload_only(tc, tile, bi)
            elif load:
                xload_halo(tc, bi)
            ji = jc[0] % 3
            jc[0] += 1
            tr.op("act", lambda e: e.activation(out=junks[ji][0:n, :], in_=xb[bi][0:n, :], func=AF.Square,
                                                accum_out=ss[0:n, c:c + 1]),
                  reads=[("xb", bi)], writes=[("junk", ji), ("ss", c)])
            tr.op("act", lambda e: e.activation(out=rst[0:n, c:c + 1], in_=ss[0:n, c:c + 1], func=AF.Identity,
                                                bias=EPS, scale=1.0 / D),
                  reads=[("ss", c)], writes=[("rst", c)])
            tr.op("pool", lambda e: e.tensor_tensor(out=rs[0:n, c:c + 1], in0=rst[0:n, c:c + 1],
                                                    in1=cst_sb[0:n, NH0:NH0 + 1], op=ALU.pow),
                  reads=[("rst", c), ("cst",)], writes=[("rs", c)])
            tr.op("dve", lambda e: e.scalar_tensor_tensor(out=xs[xsi][0:n, :], in0=xb[bi][0:n, :],
                                                          scalar=rs[0:n, c:c + 1], in1=gin_sb[0:n, :],
                                                          op0=ALU.mult, op1=ALU.mult),
                  reads=[("xb", bi), ("rs", c), ("gin", 0), ("gin", 1)], writes=[("xs", xsi)])

        Tp32 = Tp[:, :, :].rearrange("p k t -> p (k t)").bitcast(F32)

        def bfview(t):
            return t[:, :].bitcast(BF16).rearrange("p (k t) -> p k t", k=8)

        def xprep_b(tc, tile, xsi, ev="act", bank=None):
            hs = tc % 2
            n = 128 if tile < 4 else 16
            tpv, tkey = (Tp, ("Tp",)) if bank is None else bank

            def ftr(e):
                inst = None
                for k in range(8):
                    inst = e.transpose(out=tpv[:, k, 0:n], in_=xs[xsi][0:n, k * 128:(k + 1) * 128],
                                       identity=ident_b[0:n, 0:n])
                return inst
            tr.op("pe", ftr, reads=[("xs", xsi), ("identb",)], writes=[tkey])
            if tile < 4:
                dst, srcp = hT[hs][:, :, tile * 128:(tile + 1) * 128], tpv[:, :, 0:128]
            else:
                dst, srcp = hT[hs][:, :, CH:CH + 16], tpv[:, :, 0:16]
            if ev == "act":
                tr.op("act", lambda e: e.activation(out=dst, in_=srcp, func=AF.Copy),
                      reads=[tkey], writes=[("hT", hs, tile)])
            else:
                tr.op("dve", lambda e: e.tensor_copy(out=dst, in_=srcp),
                      reads=[tkey], writes=[("hT", hs, tile)])

        def wdma(dst, src, key, name):
            tr.op("pool", lambda e, s: e.dma_start(out=dst, in_=src).then_inc(s, 16),
                  writes=[key], dma=name)

        def wload_in(j, sec):
            q = j * 6 + sec
            wdma(wb[:, q, :], win[:, q, :], ("wb", q), f"wb{q}")

        def wload_out(k):
            wdma(wo[:, k, :], wout[:, k, :], ("wo", k), f"wo{k}")

        def halo_cols(t):
            return bass.AP(t, 0, [[EXT, 128], [HALO + CH, 2], [1, HALO]])

        def halo_src(bank):
            return bank[:, 0:16].rearrange("p (a b) -> p a b", a=2)

        pg = [0]

        def proj_group(tc, j, sec):
            hs = tc % 2
            b = pg[0] % 3
            pg[0] += 1
            q = j * 6 + sec

            def f(e):
                inst = None
                for k in range(8):
                    inst = e.matmul(out=P[b][:, :], lhsT=wb[:, q, k * 128:(k + 1) * 128], rhs=hT[hs][:, k, 0:CH],
                                    start=(k == 0), stop=(k == 7))
                return inst
            tr.op("pe", f, reads=[("wb", q)] + [("hT", hs, t) for t in range(4)], writes=[("P", b)])
            return b

        def halo_group(tc, j, sec, hidx, pos):
            hs = tc % 2
            bank, key = ((Hp, ("Hp",)), (Hp, ("Hp",)), (Tp32, ("Tp",)))[hidx]
            q = j * 6 + sec

            def f(e):
                inst = None
                for k in range(8):
                    inst = e.matmul(out=bank[:, 0:16], lhsT=wb[:, q, k * 128:(k + 1) * 128],
                                    rhs=hT[hs][:, k, CH:CH + 16], start=(k == 0), stop=(k == 7))
                return inst
            tr.op("pe", f, reads=[("wb", q), ("hT", hs, 4)], writes=[key])
            return bank, key

        pending_pw = [None]

        def do_pending_pw():
            if pending_pw[0] is None:
                return
            tc, g, gs = pending_pw[0]
            pending_pw[0] = None
            pslot = tc % 2
            tr.op("pe", lambda e: e.matmul(out=Mp[:, :], lhsT=pwb[:, g * 128:(g + 1) * 128], rhs=pd[pslot][:, g, :],
                                           start=True, stop=True),
                  reads=[("pwb",), ("pd", pslot, g)], writes=[("Mp",)])
            tr.op("dve", lambda e: e.scalar_tensor_tensor(out=yT[pslot][:, g, :], in0=Mp[:, :], scalar=cs(PS0 + g),
                                                          in1=sza[gs][:, :], op0=ALU.mult, op1=ALU.mult),
                  reads=[("Mp",), ("sza", gs), ("cst",)], writes=[("yT", pslot, g)])

        def proj_block(tc, pos, mid=None, late=None, mid2=None, pd_early=False):
            j = 3 - pos
            js = pos % 2
            pslot = tc % 2
            b = proj_group(tc, j, S_ZA)
            tr.op("act", lambda e, b=b: e.activation(out=sza[js][:, :], in_=P[b][:, :], func=AF.Silu),
                  reads=[("P", b)], writes=[("sza", js)])
            b = proj_group(tc, j, S_U)
            r = halo_group(tc, j, S_U, 0, pos)
            tr.op("act", lambda e, b=b: e.activation(out=u[js][:, HALO:HALO + CH], in_=P[b][:, :], func=AF.Copy),
                  reads=[("P", b)], writes=[("u", js, "m")])
            tr.op("act", lambda e, r=r: e.activation(out=halo_cols(u[js]), in_=halo_src(r[0]), func=AF.Copy),
                  reads=[r[1]], writes=[("u", js, "h")])
            w = WINDOWS[j]
            n = j + 1
            lo = HALO - w // 2
            his = [0] * (n + 2)
            his[n] = HALO + CH - w // 2
            for i in range(n - 1, 0, -1):
                his[i] = his[i + 1] + (1 << i)
            ukeys = [("u", js, "m"), ("u", js, "h")]
            src, skeys = u[js], ukeys
            bufs = [(sa, ("sa",)), (sbb, ("sbb",))]
            for i in range(1, n + 1):
                dst, dkey = bufs[(i - 1) % 2]
                sh = 1 << (i - 1)
                hi = his[i]
                tr.op("pool", lambda e, dst=dst, src=src, sh=sh, hi=hi: e.tensor_tensor(
                    out=dst[:, lo:hi], in0=src[:, lo:hi], in1=src[:, lo + sh:hi + sh], op=ALU.add),
                    reads=list(skeys), writes=[dkey])
                src, skeys = dst, [dkey]
            fin, fkeys = src, skeys
            def emit_pd():
                tr.op("dve", lambda e: e.scalar_tensor_tensor(out=pd[pslot][:, j, :], in0=fin[:, lo:lo + CH],
                                                              scalar=1.0 / w, in1=u[js][:, HALO:HALO + CH],
                                                              op0=ALU.mult, op1=ALU.subtract),
                      reads=list(fkeys) + ukeys, writes=[("pd", pslot, j)])
                for edge, cond, c0 in ((0, tc == 0, 0), (1, tc == NCH - 1, CH - 8)):
                    if not cond:
                        continue
                    rc0 = RC0 + j * 16 + edge * 8
                    tr.op("dve", lambda e, c0=c0, rc0=rc0: e.tensor_tensor(
                        out=fx[:, :], in0=fin[:, lo + c0:lo + c0 + 8], in1=cst_sb[:, rc0:rc0 + 8], op=ALU.mult),
                        reads=list(fkeys) + [("cst",)], writes=[("fx",)])
                    tr.op("dve", lambda e, c0=c0: e.tensor_tensor(
                        out=pd[pslot][:, j, c0:c0 + 8], in0=fx[:, :], in1=u[js][:, HALO + c0:HALO + c0 + 8],
                        op=ALU.subtract),
                        reads=[("fx",)] + ukeys + [("pd", pslot, j)], writes=[("pd", pslot, j)])
            if pd_early:
                emit_pd()
            if mid is not None:
                mid()
            b = proj_group(tc, j, S_ZB)
            tr.op("act", lambda e, b=b: e.activation(out=szb[js][:, :], in_=P[b][:, :], func=AF.Silu),
                  reads=[("P", b)], writes=[("szb", js)])
            b = proj_group(tc, j, S_C)
            r = halo_group(tc, j, S_C, 1, pos)
            tr.op("act", lambda e, b=b: e.activation(out=csb[js][:, HALO:HALO + CH], in_=P[b][:, :], func=AF.Copy),
                  reads=[("P", b)], writes=[("csb", js, "m")])
            tr.op("act", lambda e, r=r: e.activation(out=halo_cols(csb[js]), in_=halo_src(r[0]), func=AF.Copy),
                  reads=[r[1]], writes=[("csb", js, "h")])
            if mid2 is not None:
                mid2()
            b = proj_group(tc, j, S_V)
            r = halo_group(tc, j, S_V, 2, pos)
            tr.op("dve", lambda e, b=b: e.tensor_tensor(out=cv[js][:, HALO:HALO + CH], in0=P[b][:, :],
                                                   in1=csb[js][:, HALO:HALO + CH], op=ALU.mult),
                  reads=[("P", b), ("csb", js, "m")], writes=[("cv", js, "m")])
            tr.op("dve", lambda e, r=r: e.tensor_tensor(out=halo_cols(cv[js]), in0=halo_src(r[0]),
                                                   in1=halo_cols(csb[js]), op=ALU.mult),
                  reads=[r[1], ("csb", js, "h")], writes=[("cv", js, "h")])
            cvkeys = [("cv", js, "m"), ("cv", js, "h")]
            b = proj_group(tc, j, S_B)
            tr.op("dve", lambda e, b=b: e.tensor_tensor(out=gate[js][:, :], in0=P[b][:, :], in1=szb[js][:, :],
                                                   op=ALU.mult),
                  reads=[("P", b), ("szb", js)], writes=[("gate", js)])
            tr.op("act", lambda e: e.activation(out=t1[js][:, :], in_=cv[js][:, HALO:HALO + CH], func=AF.Identity,
                                                bias=cs(CB0 + j), scale=cs(CW0 + 4 + j)),
                  reads=cvkeys + [("cst",)], writes=[("t1", js)])
            tr.op("dve", lambda e: e.scalar_tensor_tensor(out=t2[:, :], in0=cv[js][:, HALO - 1:HALO - 1 + CH],
                                                          scalar=cs(CW0 + j), in1=t1[js][:, :],
                                                          op0=ALU.mult, op1=ALU.add),
                  reads=cvkeys + [("t1", js), ("cst",)], writes=[("t2",)])
            tr.op("dve", lambda e: e.scalar_tensor_tensor(out=t1[js][:, :], in0=cv[js][:, HALO + 1:HALO + 1 + CH],
                                                          scalar=cs(CW0 + 8 + j), in1=t2[:, :],
                                                          op0=ALU.mult, op1=ALU.add),
                  reads=cvkeys + [("t2",), ("cst",)], writes=[("t1", js)])
            tr.op("dve", lambda e: e.tensor_tensor(out=yT[pslot][:, 4 + j, :], in0=t1[js][:, :], in1=gate[js][:, :],
                                                   op=ALU.mult),
                  reads=[("t1", js), ("gate", js)], writes=[("yT", pslot, 4 + j)])
            if not pd_early:
                emit_pd()
            if late is not None:
                late()
            pending_pw[0] = (tc, j, js)

        pending_store = [None]

        def do_pending_store():
            if pending_store[0] is None:
                return
            i, s = pending_store[0]
            pending_store[0] = None
            tr.op("act", lambda e, sm: e.dma_start(out=y[i * 128:(i + 1) * 128, :], in_=xb[s][:, :]).then_inc(sm, 16),
                  reads=[("xb", s)], writes=[("y", i)], dma=f"yo{s}")

        def out_slot(i):
            if i >= 12:
                return (0, 1, 2, 3)[i - 12]
            return 2 + (i % 3)

        loaded = set()

        def out_load(i):
            if i in loaded or i >= 4 * NCH:
                return
            loaded.add(i)
            s = out_slot(i)
            r0 = HALO + i * 128
            tr.op("sp", lambda e, sm: e.dma_start(out=xb[s][:, :], in_=x[r0:r0 + 128, :]).then_inc(sm, 16),
                  writes=[("xb", s)], dma=f"xb{s}")

        pending_ew = []

        def out_tile_pe(tc, ti, banks=None):
            i = tc * 4 + ti
            ys = tc % 2
            if banks is None:
                banks = ((O[0], ("O", 0)), (O[1], ("O", 1)))

            rd = [("yT", ys, e_) for e_ in range(8)] + [("wo", k) for k in range(8)]
            if i == 4 * NCH - 1:
                for dh in range(2):
                    def fh(e, dh=dh):
                        inst = None
                        for ec in range(8):
                            inst = e.matmul(out=banks[dh][0][:, :], lhsT=yT[ys][:, ec, ti * 128:(ti + 1) * 128],
                                            rhs=wo[:, ec, dh * CH:(dh + 1) * CH], start=(ec == 0), stop=(ec == 7))
                        return inst
                    tr.op("pe", fh, reads=rd, writes=[banks[dh][1]])
            else:
                def f(e):
                    inst = None
                    for ec in range(8):
                        for dh in range(2):
                            inst = e.matmul(out=banks[dh][0][:, :], lhsT=yT[ys][:, ec, ti * 128:(ti + 1) * 128],
                                            rhs=wo[:, ec, dh * CH:(dh + 1) * CH], start=(ec == 0), stop=(ec == 7))
                    return inst
                tr.op("pe", f, reads=rd, writes=[banks[0][1], banks[1][1]])
            pending_ew.append((i, banks))

        pending_ewb = []

        def ew_a():
            if not pending_ew:
                return
            i, banks = pending_ew.pop(0)
            s = out_slot(i)
            for dh in range(2):
                tr.op("dve", lambda e, dh=dh: e.tensor_tensor(out=xb[s][:, dh * CH:(dh + 1) * CH],
                                                              in0=banks[dh][0][:, :],
                                                              in1=xb[s][:, dh * CH:(dh + 1) * CH], op=ALU.add),
                      reads=[banks[dh][1], ("xb", s)], writes=[("xb", s)])
            ji = jc[0] % 3
            jc[0] += 1
            tr.op("act", lambda e: e.activation(out=junks[ji][:, :], in_=xb[s][:, :], func=AF.Square,
                                                accum_out=ss2[:, i:i + 1]),
                  reads=[("xb", s)], writes=[("junk", ji), ("ss2", i)])
            tr.op("pool", lambda e: e.tensor_scalar(rst2[:, i:i + 1], ss2[:, i:i + 1], 1.0 / D, EPS,
                                                    ALU.mult, ALU.add),
                  reads=[("ss2", i)], writes=[("rst2", i)])
            tr.op("pool", lambda e: e.tensor_tensor(out=rs2[:, i:i + 1], in0=rst2[:, i:i + 1],
                                                    in1=cst_sb[:, NH0:NH0 + 1], op=ALU.pow),
                  reads=[("rst2", i), ("cst",)], writes=[("rs2", i)])
            pending_ewb.append(i)

        def ew_b():
            if not pending_ewb:
                return
            i = pending_ewb.pop(0)
            s = out_slot(i)
            if i == 4 * NCH - 1:
                do_pending_store()
                for h, q in ((0, "act"), (1, "sp")):
                    cs_ = slice(h * CH, (h + 1) * CH)
                    tr.op("dve", lambda e, cs_=cs_: e.scalar_tensor_tensor(
                        out=xb[s][:, cs_], in0=xb[s][:, cs_], scalar=rs2[:, i:i + 1], in1=gfin_sb[:, cs_],
                        op0=ALU.mult, op1=ALU.mult),
                        reads=[("xb", s), ("rs2", i), ("gfin",)], writes=[("xbh", h)])
                    tr.op(q, lambda e, sm, cs_=cs_: e.dma_start(out=y[i * 128:(i + 1) * 128, cs_],
                                                              in_=xb[s][:, cs_]).then_inc(sm, 16),
                          reads=[("xbh", h)], writes=[("y", i, h)], dma=f"yoL{h}")
                return
            tr.op("dve", lambda e: e.scalar_tensor_tensor(out=xb[s][:, :], in0=xb[s][:, :], scalar=rs2[:, i:i + 1],
                                                          in1=gfin_sb[:, :], op0=ALU.mult, op1=ALU.mult),
                  reads=[("xb", s), ("rs2", i), ("gfin",)], writes=[("xb", s)])
            do_pending_store()
            pending_store[0] = (i, s)

        def flush_ew():
            ew_a()
            ew_b()

        warm = sb("warm", [128, 8])
        tr.op("pool", lambda e: e.memset(warm[:, :], 0.0), writes=[("warm",)])
        tr.op("act", lambda e: e.activation(out=warm[:, :], in_=warm[:, :], func=AF.Silu),
              reads=[("warm",)], writes=[("warm",)])
        worder = [S_ZA, S_U, S_ZB, S_C, S_V, S_B]
        gorder = [3, 2, 1, 0]

        ld(grow[:, :], gin[:, :], ("grow",), "grow")
        xload_only(0, 0, 0)
        ld(cst_sb[:, :], cst[:, :], ("cst",), "cst")
        ld(ident_f[:, :], ident[:, :], ("identf",), "identf")
        xload_only(0, 1, 1)
        xload_halo(0, 4)
        xload_only(0, 2, 2)
        xload_only(0, 3, 3)
        tr.op("pool", lambda e: e.memset(ones[:, :], 1.0), writes=[("ones",)])
        for hh in range(2):
            tr.op("pe", lambda e, hh=hh: e.matmul(out=P[hh][:, :], lhsT=ones[0:1, :],
                                                  rhs=grow[0:1, hh * CH:(hh + 1) * CH], start=True, stop=True),
                  reads=[("ones",), ("grow",)], writes=[("P", hh)])
            tr.op("dve", lambda e, hh=hh: e.tensor_copy(out=gin_sb[:, hh * CH:(hh + 1) * CH], in_=P[hh][:, :]),
                  reads=[("P", hh)], writes=[("gin", hh)])
        xprep_a(0, 0, 0, 0, load=False)
        tr.op("act", lambda e: e.activation(out=ident_b[:, :], in_=ident_f[:, :], func=AF.Copy),
              reads=[("identf",)], writes=[("identb",)])
        xprep_a(0, 1, 1, 1, load=False)
        xprep_b(0, 0, 0, "act")
        xprep_a(0, 4, 4, 0, load=False)
        q0 = gorder[0] * 6 + worder[0]
        tr.op("pool", lambda e, s: e.dma_start(out=wb[:, q0, :], in_=win[:, q0, :]).then_inc(s, 16),
              reads=[("xb", 2)], writes=[("wb", q0)], dma=f"wb{q0}")
        xprep_a(0, 2, 2, 2, load=False)
        xprep_b(0, 1, 1, "act", (bfview(O[0]), ("O", 0)))
        xprep_a(0, 3, 3, 3, load=False)
        first = True
        for sec in worder[1:]:
            q = gorder[0] * 6 + sec
            tr.op("pool", lambda e, s, q=q: e.dma_start(out=wb[:, q, :], in_=win[:, q, :]).then_inc(s, 16),
                  reads=[("xb", 3)] if first else [], writes=[("wb", q)], dma=f"wb{q}")
            first = False
        wdma(pwb[:, :], pw[:, :], ("pwb",), "pwb")
        xprep_b(0, 4, 0, "dve", (bfview(O[1]), ("O", 1)))
        xprep_b(0, 2, 2, "act", (bfview(Hp), ("Hp",)))
        xprep_b(0, 3, 3, "dve")
        ld(gfin_sb[:, :], gfin[:, :], ("gfin",), "gfin")

        def btiles(bidx):
            tc_, j_ = divmod(bidx, 4)
            if tc_ >= NCH - 1 or bidx < 0:
                return []
            return [[], [(tc_ + 1, 0), (tc_ + 1, 4)], [(tc_ + 1, 1), (tc_ + 1, 2)], [(tc_ + 1, 3)]][j_]

        xbc = [0]
        xsc = [1]
        prepared = {}

        def do_a(tiles):
            for (tc_, tile) in tiles:
                bi = xbc[0] % 2
                xbc[0] += 1
                xsi = xsc[0] % NXS
                xsc[0] += 1
                xprep_a(tc_, tile, bi, xsi)
                prepared[(tc_, tile)] = xsi

        do_a(btiles(0))
        for tc in range(NCH):
            for j in range(4):
                bidx = tc * 4 + j
                do_a(btiles(bidx + 1))
                if tc >= 1:
                    out_load((tc - 1) * 4 + j)
                if bidx == 14:
                    out_load(12)
                    out_load(13)
                if tc == 0 and j < 3:
                    for sec in worder:
                        wload_in(gorder[j + 1], sec)
                if tc == 0 and j == 3:
                    for k in range(8):
                        wload_out(k)

                def mid(bidx=bidx):
                    if pending_pw[0] is not None and pending_pw[0][0] not in (0, NCH - 1):
                        do_pending_pw()
                    for (tc_, tile) in btiles(bidx)[:1]:
                        xprep_b(tc_, tile, prepared[(tc_, tile)])
                    flush_ew()
                    if bidx == 15:
                        out_load(14)

                def mid2(bidx=bidx):
                    if pending_pw[0] is not None and (pending_pw[0][1] == 0 or bidx == 4 * NCH - 1):
                        do_pending_pw()
                    for (tc_, tile) in btiles(bidx)[1:]:
                        xprep_b(tc_, tile, prepared[(tc_, tile)])
                last = (bidx == 4 * NCH - 1)
                proj_block(tc, j, mid, do_pending_pw, mid2, pd_early=last)
                if last:
                    do_pending_pw()
                if tc >= 1:
                    out_tile_pe(tc - 1, j)
        do_pending_pw()
        tail_banks = [
            ((P[0], ("P", 0)), (P[1], ("P", 1))),
            ((P[2], ("P", 2)), (Hp, ("Hp",))),
            ((O[0], ("O", 0)), (O[1], ("O", 1))),
            ((P[0], ("P", 0)), (P[1], ("P", 1))),
        ]
        flush_ew()
        out_load(15)
        for ti in range(3):
            out_tile_pe(NCH - 1, ti, tail_banks[ti])
        ew_a()
        out_tile_pe(NCH - 1, 3, tail_banks[3])
        ew_a()
        ew_b()
        ew_a()
        ew_b()
        ew_a()
        ew_b()
        ew_b()
        do_pending_store()

        final_waits = [(ent[0], ent[1]) for name, ent in tr.dsem.items() if name.startswith("yo")]
        tr.finalize()


        with nc.Block() as block:
            @block.sync
            def _(e):
                tr.emit("sp", e)

            @block.tensor
            def _(e):
                tr.emit("pe", e)

            @block.scalar
            def _(e):
                tr.emit("act", e, final_waits)

            @block.vector
            def _(e):
                tr.emit("dve", e)

            @block.gpsimd
            def _(e):
                tr.emit("pool", e)
    return nc


_NC_CACHE = {}


def _layout_inputs(x, norm_g, w_in, pool_w, pool_scale, conv_w, conv_b, w_out, final_g):
    f = np.float32
    x = np.asarray(x, f)
    w = np.asarray(w_in, f)[0]
    w = w.reshape(8, 128, 6, 4, 128)
    win = np.ascontiguousarray(w.transpose(1, 3, 2, 0, 4)).reshape(128, 24, 1024)
    wout = np.ascontiguousarray(np.asarray(w_out, f)[0].reshape(8, 128, 1024).transpose(1, 0, 2))
    pw = np.ascontiguousarray(np.asarray(pool_w, f)[0].transpose(1, 0, 2)).reshape(128, 512)
    gin = np.ascontiguousarray(np.asarray(norm_g, f)[0][None, :])
    ident = np.eye(128, dtype=f)
    gfin = np.ascontiguousarray(np.broadcast_to(np.asarray(final_g, f)[None, :], (128, D)))
    cst_base = np.zeros((128, NCST), f)
    cst_base[:, G0:G0 + 8] = np.asarray(norm_g, f)[0].reshape(8, 128).T
    cst_base[:, PS0:PS0 + 4] = np.asarray(pool_scale, f)[0].reshape(4, 128).T
    cw = np.asarray(conv_w, f)[0].reshape(3, 4, 128)
    cst_base[:, CW0:CW0 + 12] = cw.transpose(2, 0, 1).reshape(128, 12)
    cst_base[:, CB0:CB0 + 4] = np.asarray(conv_b, f)[0].reshape(4, 128).T
    cst_base[:, NH0] = -0.5
    in_maps = []
    for c in range(NCORES):
        b, q = divmod(c, SEQ // T)
        t0 = q * T
        xe = np.zeros((T + 2 * HALO, D), f)
        lo = max(t0 - HALO, 0)
        hi = min(t0 + T + HALO, SEQ)
        xe[lo - (t0 - HALO):hi - (t0 - HALO)] = x[b, lo:hi]
        cst = cst_base.copy()
        for g, wdw in enumerate(WINDOWS):
            for edge in range(2):
                for i in range(8):
                    t = t0 + i if edge == 0 else t0 + T - 8 + i
                    cnt = min(t + wdw // 2, SEQ) - max(t - wdw // 2, 0)
                    cst[:, RC0 + g * 16 + edge * 8 + i] = f(1.0) / f(cnt)
        in_maps.append({"x": xe, "win": win, "wout": wout, "pw": pw, "cst": cst,
                        "ident": ident, "gfin": gfin, "gin": gin})
    return in_maps


def kernel(x, norm_g, w_in, pool_w, pool_scale, conv_w, conv_b, w_out, final_g):
    in_maps = _layout_inputs(x, norm_g, w_in, pool_w, pool_scale, conv_w, conv_b, w_out, final_g)
    if "nc" not in _NC_CACHE:
        _NC_CACHE["nc"] = build_nc()
    nc = _NC_CACHE["nc"]
    res = run_bass_kernel_spmd(nc, in_maps, core_ids=list(range(NCORES)))
    out = np.empty((2, SEQ, D), np.float32)
    for c in range(NCORES):
        b, q = divmod(c, SEQ // T)
        out[b, q * T:(q + 1) * T] = res.results[c]["y"]
    return out
```
